# Optimizing a Trainium2 kernel written in Bass

```python
import jax, jax.numpy as jnp
from jax import lax
import numpy as np

D_MODEL = 2048
BATCH = 4
SEQ = 8192
DEPTH = 1

GRID_W = 64
CTX_LEN = 256
NH_A = 8
DK_A = 128
DV_A = 256
QK_A = NH_A * DK_A
V_A = NH_A * DV_A
CONV_W = 3
CHUNK = 64
M_INIT = -1e30
NH_B = 16
NKV_B = 4
HD_B = 128
Q_B = NH_B * HD_B
KV_B = NKV_B * HD_B
ROT_HALF = HD_B // 2
ROPE_THETA = 10000.0
Q_BLOCK = 128
EPS = 1e-6
ALPHA = (2 * DEPTH) ** 0.25
BETA = (8 * DEPTH) ** -0.25
KV_WIDTHS = (2 * QK_A, V_A, 4 * NH_A, KV_B, KV_B)
N_KV = 2 * QK_A + V_A + 4 * NH_A + 2 * KV_B
OUT_WIDTHS = (V_A, V_A, Q_B, Q_B, 2 * D_MODEL)
N_IN = N_KV + 2 * V_A + 2 * Q_B + 2 * D_MODEL

kernel_name = "hybrid_mlstm_gqa_dit_block"


def _split(p, widths, start=0):
    outs = []
    off = start
    for w in widths:
        outs.append(p[..., off:off + w])
        off += w
    return outs


def layer_norm(x, w=None, b=None):
    xf = x.astype(jnp.float32)
    mu = xf.mean(-1, keepdims=True)
    var = jnp.mean(jnp.square(xf - mu), -1, keepdims=True)
    y = (xf - mu) * lax.rsqrt(var + EPS)
    if w is not None:
        y = y * w.astype(jnp.float32) + b.astype(jnp.float32)
    return y.astype(x.dtype)


def rms_norm(x, w):
    xf = x.astype(jnp.float32)
    y = xf * lax.rsqrt(jnp.mean(jnp.square(xf), -1, keepdims=True) + EPS)
    return (y * w.astype(jnp.float32)).astype(x.dtype)


def dwconv_centred(x, w, b):
    T = x.shape[1]
    pad = CONV_W // 2
    xp = jnp.pad(x, ((0, 0), (pad, CONV_W - 1 - pad), (0, 0)))
    y = b
    for j in range(CONV_W):
        y = y + xp[:, j:j + T] * w[j]
    return y


def rope_tables(n):
    rows_n = n // GRID_W
    row = jnp.repeat(jnp.arange(rows_n), GRID_W).astype(jnp.float32)
    col = jnp.tile(jnp.arange(GRID_W), rows_n).astype(jnp.float32)
    inv = ROPE_THETA ** (-jnp.arange(0, ROT_HALF, 2, dtype=jnp.float32) / ROT_HALF)
    ang_r = row[:, None] * inv[None]
    ang_c = col[:, None] * inv[None]
    return (jnp.cos(ang_r), jnp.sin(ang_r), jnp.cos(ang_c), jnp.sin(ang_c))


def _rot(xh, cos, sin):
    h = xh.shape[-1] // 2
    x1, x2 = xh[..., :h], xh[..., h:]
    cos = cos[None, :, None, :]
    sin = sin[None, :, None, :]
    return jnp.concatenate([x1 * cos - x2 * sin, x1 * sin + x2 * cos], axis=-1)


def apply_rope_2d(x, rope):
    cr, sr, cc, sc = rope
    xf = x.astype(jnp.float32)
    y = jnp.concatenate([_rot(xf[..., :ROT_HALF], cr, sr),
                         _rot(xf[..., ROT_HALF:], cc, sc)], axis=-1)
    return y.astype(x.dtype)


def zero_state(b):
    return (jnp.zeros((b, NH_A, DV_A, DK_A), jnp.float32),
            jnp.zeros((b, NH_A, DK_A), jnp.float32),
            jnp.full((b, NH_A), M_INIT, jnp.float32))


def mlstm_chunked(q, k, v, log_i, log_f, state):
    B, T, H, _ = q.shape
    nc = T // CHUNK

    def to_chunks(a):
        a = a.reshape((B, nc, CHUNK, H) + a.shape[3:])
        return jnp.moveaxis(a, (1, 3), (0, 2))

    tril = jnp.tril(jnp.ones((CHUNK, CHUNK), bool))

    def step(carry, xs):
        C0, n0, m0 = carry
        qc, kc, vc, ic, fc = xs
        b = jnp.cumsum(fc, axis=-1)
        d = jnp.where(tril, b[..., :, None] - b[..., None, :] + ic[..., None, :], -jnp.inf)
        m_inter = b + m0[..., None]
        m = jnp.maximum(m_inter, d.max(-1))
        w = jnp.exp(d - m[..., None])
        a = jnp.exp(m_inter - m)
        s = jnp.einsum('bhjd,bhsd->bhjs', qc, kc) * w
        num = (a[..., None] * jnp.einsum('bhvd,bhjd->bhjv', C0, qc)
               + jnp.einsum('bhjs,bhsv->bhjv', s, vc))
        den = a * jnp.einsum('bhd,bhjd->bhj', n0, qc) + s.sum(-1)
        h = num / jnp.maximum(jnp.abs(den), jnp.exp(-m))[..., None]
        m_end = m[..., -1]
        w_end = jnp.exp(b[..., -1:] - b + ic - m_end[..., None])
        a_end = a[..., -1]
        C = a_end[..., None, None] * C0 + jnp.einsum('bhs,bhsv,bhsd->bhvd', w_end, vc, kc)
        n = a_end[..., None] * n0 + jnp.einsum('bhs,bhsd->bhd', w_end, kc)
        return (C, n, m_end), h

    xs = tuple(to_chunks(a) for a in (q, k, v, log_i, log_f))
    state, h = lax.scan(step, state, xs)
    h = jnp.moveaxis(h, (0, 2), (1, 3)).reshape(B, T, H, v.shape[-1])
    return h, state


def mlstm_final_state(k, v, log_i, log_f):
    b = jnp.cumsum(log_f, axis=1)
    g = b[:, -1:] - b + log_i
    m = g.max(axis=1)
    w = jnp.exp(g - m[:, None])
    C = jnp.einsum('bth,bthv,bthd->bhvd', w, v, k)
    n = jnp.einsum('bth,bthd->bhd', w, k)
    return (C, n, m)


def _flip(*arrs):
    return [jnp.flip(a, axis=1) for a in arrs]


def mlstm_inputs(qk_pre, v_a, if_a, conv_w, conv_b, b_if):
    B, T = v_a.shape[:2]
    qk = jax.nn.silu(dwconv_centred(qk_pre, conv_w, conv_b)).astype(jnp.float32)
    q = qk[..., :QK_A].reshape(B, T, NH_A, DK_A)
    k = qk[..., QK_A:].reshape(B, T, NH_A, DK_A) * (DK_A ** -0.5)
    v = v_a.astype(jnp.float32).reshape(B, T, NH_A, DV_A)
    gt = (if_a + b_if).astype(jnp.float32).reshape(B, T, 4, NH_A)
    fwd = (gt[:, :, 0], jax.nn.log_sigmoid(gt[:, :, 1]))
    bwd = (gt[:, :, 2], jax.nn.log_sigmoid(gt[:, :, 3]))
    return q, k, v, fwd, bwd


def attn_kv(k_b, v_b, k_norm_w, rope):
    B, T = k_b.shape[:2]
    k = rms_norm(k_b.reshape(B, T, NKV_B, HD_B), k_norm_w)
    if rope is not None:
        k = apply_rope_2d(k, rope)
    return k, v_b.reshape(B, T, NKV_B, HD_B)


def attend_blocks(q, k_all, v_all):
    B, S = q.shape[:2]
    G = NH_B // NKV_B
    nb = S // Q_BLOCK
    qb = q.reshape(B, nb, Q_BLOCK, NKV_B, G, HD_B).transpose(1, 0, 2, 3, 4, 5)
    scale = HD_B ** -0.5

    def one(qblk):
        s = jnp.einsum('bqkgd,btkd->bkgqt', qblk, k_all).astype(jnp.float32) * scale
        p = jax.nn.softmax(s, axis=-1).astype(v_all.dtype)
        return jnp.einsum('bkgqt,btkd->bqkgd', p, v_all)

    o = lax.map(one, qb)
    return o.transpose(1, 0, 2, 3, 4, 5).reshape(B, S, Q_B)


def branch_merge(h_a, o_attn, o_a, z_a, z_b, g_logits, mh_norm_w, w_ba, w_bb, w_out):
    B, T = h_a.shape[:2]
    h = rms_norm(h_a, mh_norm_w.reshape(NH_A, DV_A)).reshape(B, T, V_A)
    y_a = (jax.nn.sigmoid(o_a) * h * jax.nn.silu(z_a)) @ w_ba
    y_b = (o_attn * jax.nn.silu(z_b)) @ w_bb
    g_a, g_b = jnp.split(g_logits, 2, axis=-1)
    return (jax.nn.sigmoid(g_a) * y_a + jax.nn.sigmoid(g_b) * y_b) @ w_out


def trunk_layer(x, ctx, c, c_ctx, rope, w_mod, b_mod, w_in, b_if, conv_w, conv_b,
                mh_norm_w, q_norm_w, k_norm_w, w_ba, w_bb, w_out, ln_w, ln_b, update_ctx):
    B, S = x.shape[:2]
    T_c = ctx.shape[1]
    shift, scale, gate = [m[:, None, :] for m in jnp.split(jax.nn.silu(c) @ w_mod + b_mod, 3, axis=-1)]
    shift_c, scale_c, gate_c = jnp.split(jax.nn.silu(c_ctx) @ w_mod + b_mod, 3, axis=-1)
    u = layer_norm(x) * (1 + scale) + shift
    u_c = layer_norm(ctx) * (1 + scale_c) + shift_c
    p = u @ w_in
    p_c = u_c @ (w_in if update_ctx else w_in[:, :N_KV])

    qk_c, va_c, if_c, kb_c, vb_c = _split(p_c, KV_WIDTHS)
    qc, kc, vc, gf_c, gb_c = mlstm_inputs(qk_c, va_c, if_c, conv_w, conv_b, b_if)
    k_bc, v_bc = attn_kv(kb_c, vb_c, k_norm_w, None)
    if update_ctx:
        h_cf, st_f = mlstm_chunked(qc, kc, vc, *gf_c, zero_state(B))
        h_cb, st_b = mlstm_chunked(*_flip(qc, kc, vc, *gb_c), zero_state(B))
        h_c = (h_cf + jnp.flip(h_cb, axis=1)).astype(ctx.dtype)
        o_ac, z_ac, q_bc, z_bc, g_c = _split(p_c, OUT_WIDTHS, N_KV)
        q_bc = rms_norm(q_bc.reshape(B, T_c, NH_B, HD_B), q_norm_w)
        o_attn_c = attend_blocks(q_bc, k_bc, v_bc)
        out_c = branch_merge(h_c, o_attn_c, o_ac, z_ac, z_bc, g_c, mh_norm_w, w_ba, w_bb, w_out)
        ctx_new = layer_norm(ALPHA * ctx + gate_c * out_c, ln_w, ln_b)
    else:
        st_f = mlstm_final_state(kc, vc, *gf_c)
        st_b = mlstm_final_state(*_flip(kc, vc, *gb_c))
        ctx_new = ctx

    qk_l, va_l, if_l, kb_l, vb_l = _split(p, KV_WIDTHS)
    o_a, z_a, q_b, z_b, g_l = _split(p, OUT_WIDTHS, N_KV)
    ql, kl, vl, gf_l, gb_l = mlstm_inputs(qk_l, va_l, if_l, conv_w, conv_b, b_if)
    h_f, _ = mlstm_chunked(ql, kl, vl, *gf_l, st_f)
    h_b, _ = mlstm_chunked(*_flip(ql, kl, vl, *gb_l), st_b)
    h_l = (h_f + jnp.flip(h_b, axis=1)).astype(x.dtype)

    q_l = apply_rope_2d(rms_norm(q_b.reshape(B, S, NH_B, HD_B), q_norm_w), rope)
    k_bl, v_bl = attn_kv(kb_l, vb_l, k_norm_w, rope)
    k_all = jnp.concatenate([k_bc, k_bl], axis=1)
    v_all = jnp.concatenate([v_bc, v_bl], axis=1)
    o_attn = attend_blocks(q_l, k_all, v_all)

    out = branch_merge(h_l, o_attn, o_a, z_a, z_b, g_l, mh_norm_w, w_ba, w_bb, w_out)
    x_new = layer_norm(ALPHA * x + gate * out, ln_w, ln_b)
    return x_new, ctx_new


def setup_inputs(seed: int = 0) -> dict:
    key = jax.random.key(seed)
    ks = jax.random.split(key, 20)
    D = D_MODEL
    nrm = jax.random.normal
    b_if_i = 0.1 * nrm(ks[8], (DEPTH, 2, 1, NH_A))
    b_if_f = 3.0 + 0.5 * nrm(ks[9], (DEPTH, 2, 1, NH_A))
    b_if = jnp.concatenate([b_if_i, b_if_f], axis=2).reshape(DEPTH, 4 * NH_A)
    return {
        "x": nrm(ks[0], (BATCH, SEQ, D), jnp.float32),
        "c": nrm(ks[1], (BATCH, D), jnp.float32),
        "ctx": nrm(ks[2], (BATCH, CTX_LEN, D), jnp.float32),
        "c_ctx": nrm(ks[3], (D,), jnp.float32),
        "w_mod": 0.5 * D ** -0.5 * nrm(ks[4], (DEPTH, D, 3 * D), jnp.float32),
        "b_mod": 0.01 * nrm(ks[5], (DEPTH, 3 * D), jnp.float32),
        "w_in": D ** -0.5 * nrm(ks[6], (DEPTH, D, N_IN), jnp.float32),
        "b_if": b_if.astype(jnp.float32),
        "conv_w": CONV_W ** -0.5 * nrm(ks[7], (DEPTH, CONV_W, 2 * QK_A), jnp.float32),
        "conv_b": 0.01 * nrm(ks[10], (DEPTH, 2 * QK_A), jnp.float32),
        "mh_norm_w": 1.0 + 0.02 * nrm(ks[11], (DEPTH, V_A), jnp.float32),
        "q_norm_w": 1.0 + 0.02 * nrm(ks[12], (DEPTH, HD_B), jnp.float32),
        "k_norm_w": 1.0 + 0.02 * nrm(ks[13], (DEPTH, HD_B), jnp.float32),
        "w_branch_a": BETA * V_A ** -0.5 * nrm(ks[14], (DEPTH, V_A, D), jnp.float32),
        "w_branch_b": BETA * Q_B ** -0.5 * nrm(ks[15], (DEPTH, Q_B, D), jnp.float32),
        "w_out": BETA * D ** -0.5 * nrm(ks[16], (DEPTH, D, D), jnp.float32),
        "ln_w": 1.0 + 0.02 * nrm(ks[17], (DEPTH, D), jnp.float32),
        "ln_b": 0.01 * nrm(ks[18], (DEPTH, D), jnp.float32),
    }


def reference(x, c, ctx, c_ctx, w_mod, b_mod, w_in, b_if, conv_w, conv_b, mh_norm_w,
              q_norm_w, k_norm_w, w_branch_a, w_branch_b, w_out, ln_w, ln_b):
    rope = rope_tables(x.shape[1])
    for layer in range(DEPTH):
        x, ctx = trunk_layer(x, ctx, c, c_ctx, rope, w_mod[layer], b_mod[layer], w_in[layer],
                             b_if[layer], conv_w[layer], conv_b[layer], mh_norm_w[layer],
                             q_norm_w[layer], k_norm_w[layer], w_branch_a[layer],
                             w_branch_b[layer], w_out[layer], ln_w[layer], ln_b[layer],
                             layer < DEPTH - 1)
    return x
```

```python
import math
from contextlib import ExitStack
import numpy as np
import ml_dtypes
import concourse.bass as bass
import concourse.mybir as mybir
from concourse.bass_utils import run_bass_kernel_spmd

F32 = mybir.dt.float32
BF16 = mybir.dt.bfloat16
AF = mybir.ActivationFunctionType
ALU = mybir.AluOpType
AX = mybir.AxisListType

D = 2048
KC = 16
NH_A = 8
DK_A = 128
DV_A = 256
QK_A = 1024
V_A = 2048
NH_B = 16
NKV_B = 4
HD_B = 128
Q_B = 2048
KV_B = 512
GRID_W = 64
CHUNK = 64
EPS = 1e-6
DEPTH = 1
ALPHA = (2 * DEPTH) ** 0.25
ROPE_THETA = 10000.0
N_KV = 2 * QK_A + V_A + 4 * NH_A + 2 * KV_B
N_IN = N_KV + 2 * V_A + 2 * Q_B + 2 * D
C_QA = 0
C_KA = 1024
C_VA = 2048
C_IF = 4096
C_KB = 4128
C_VB = 4640
C_OA = 5152
C_ZA = 7200
C_QB = 9248
C_ZB = 11296
C_GA = 13344
C_GB = 15392


class Obj:
    def __init__(self, name):
        self.name = name
        self.w = []
        self.r = []
        self.dsem = None
        self.dval = 0


class Sy:
    LIMIT = 30000

    def __init__(self, nc):
        self.nc = nc
        self.eng = {'pe': nc.tensor, 'dve': nc.vector, 'act': nc.scalar, 'pool': nc.gpsimd, 'sp': nc.sync}
        self.cur = {}
        self.waited = {e: {} for e in self.eng}
        self.nsem = 0
        self.ninst = 0
        self.dfinal = {}
        for e in self.eng:
            self._new_eng_sem(e)

    def _alloc(self, name):
        s = self.nc.alloc_semaphore(name=name)
        self.nsem += 1
        return s

    def _new_eng_sem(self, e):
        self.cur[e] = [self._alloc(f"s_{e}_{self.nsem}"), 0]

    def _wait(self, e, deps):
        best = {}
        for (s, v) in deps:
            k = id(s)
            if k not in best or best[k][1] < v:
                best[k] = (s, v)
        for k, (s, v) in best.items():
            if self.waited[e].get(k, 0) >= v:
                continue
            if e == 'pe' and s is self.cur['pe'][0] and v > self.cur['pe'][1]:
                continue
            self.eng[e].wait_ge(s, v)
            self.ninst += 1
            self.waited[e][k] = v

    @staticmethod
    def _trim(lst):
        best = {}
        for (s, v) in lst:
            k = id(s)
            if k not in best or best[k][1] < v:
                best[k] = (s, v)
        return list(best.values())

    def _deps(self, reads, writes):
        deps = []
        for o in reads:
            deps += o.w
        for o in writes:
            deps += o.w
            deps += o.r
        return deps

    def _record(self, tok, reads, writes):
        for o in reads:
            o.r.append(tok)
            if len(o.r) > 8:
                o.r = self._trim(o.r)
        for o in writes:
            o.w = [tok]
            o.r = []

    def op(self, e, fn, reads=(), writes=()):
        self._wait(e, self._deps(reads, writes))
        inst = fn()
        self.ninst += 1
        c = self.cur[e]
        c[1] += 1
        inst.then_inc(c[0], 1)
        self._record((c[0], c[1]), reads, writes)
        if c[1] >= self.LIMIT:
            self._new_eng_sem(e)
        return inst

    def mm(self, fn, reads=(), writes=(), last=True):
        e = 'pe'
        self._wait(e, self._deps(reads, writes))
        inst = fn()
        self.ninst += 1
        if last:
            c = self.cur[e]
            c[1] += 1
            inst.then_inc(c[0], 1)
            self._record((c[0], c[1]), reads, writes)
            if c[1] >= self.LIMIT:
                self._new_eng_sem(e)
        else:
            c = self.cur[e]
            self._record((c[0], c[1] + 1), reads, writes)
        return inst

    def dma(self, q, out_ap, in_ap, sb, reads=(), writes=(), **kw):
        self._wait(q, self._deps(reads, writes))
        if sb.dsem is None or sb.dval >= 16 * 2000:
            sb.dsem = self._alloc(f"d_{sb.name}_{self.nsem}")
            sb.dval = 0
        inst = self.eng[q].dma_start(out=out_ap, in_=in_ap, **kw)
        self.ninst += 1
        sb.dval += 16
        inst.then_inc(sb.dsem, 16)
        self.dfinal[id(sb.dsem)] = (sb.dsem, sb.dval)
        self._record((sb.dsem, sb.dval), reads, writes)
        return inst

    def barrier(self, objs):
        deps = []
        for e, c in self.cur.items():
            if c[1] > 0:
                deps.append((c[0], c[1]))
        for o in objs:
            deps += o.w + o.r
        deps += list(self.dfinal.values())
        for e in self.eng:
            self._wait(e, deps)

    def finish(self, objs):
        self.barrier(objs)


_UN = {'n': 0}


def _uniq(name):
    _UN['n'] += 1
    return "t%d_%s" % (_UN['n'], name)


def _groups(n0, n1, g=512):
    out = []
    t = n0
    while t < n1:
        n = min(g, n1 - t)
        out.append((t, n))
        t += n
    return out


def build(OWN, OTH, CTX):
    NLAT = OWN + OTH
    NT = NLAT + CTX
    NCH = NT // CHUNK
    assert OWN % 512 == 0 and OTH % 512 == 0 and CTX % 128 == 0 and CTX <= 512
    nc = bass.Bass("TRN2", target_bir_lowering=False)

    def din(name, shape, dt=F32):
        return nc.dram_tensor(name, list(shape), dt, kind="ExternalInput").ap()

    def dscr(name, shape, dt):
        return nc.dram_tensor(name, list(shape), dt).ap()

    xa = din("xa", [NT, D])
    cvec = din("cvec", [128, KC, 2])
    w_mod = din("w_mod", [D, 3 * D])
    bmod_t = din("bmod_t", [128, 48])
    bmod_g = din("bmod_g", [2, D])
    w_in = din("w_in", [D, N_IN])
    b_if = din("b_if", [1, 32])
    convw_t = din("convw_t", [128, KC, 3])
    convb_t = din("convb_t", [128, KC])
    mhw_t = din("mhw_t", [128, KC])
    qnw = din("qnw", [1, 128])
    knw = din("knw", [1, 128])
    w_ba = din("w_ba", [D, D])
    w_bb = din("w_bb", [D, D])
    w_out = din("w_out", [D, D])
    ln_w = din("ln_w", [1, D])
    ln_b = din("ln_b", [1, D])
    rope_c = din("rope_c", [NT, 128])
    rope_s = din("rope_s", [NT, 128])
    ident_in = din("ident", [128, 128])
    tri_in = din("tri", [4, 64, 64])
    yout = nc.dram_tensor("yout", [OWN, D], F32, kind="ExternalOutput").ap()

    UT = dscr("UT", [(NT + 511) // 512, 128, KC * 512], BF16)
    QKT = dscr("QKT", [2048, NT], BF16)
    SG = dscr("SG", [5 * 2048, OWN], BF16)
    VA = dscr("VA", [NT, 2048], BF16)
    VB = dscr("VB", [NT, 512], BF16)
    GT = dscr("GT", [NT, 32], F32)
    QBT = dscr("QBT", [NH_B * 128, OWN], BF16)
    KBT = dscr("KBT", [NKV_B * 128, NT], BF16)
    HF = dscr("HF", [OWN, 2048], F32)
    HN = dscr("HN", [OWN, 2048], BF16)
    GA = dscr("GA", [2048, OWN], BF16)
    GB = dscr("GB", [2048, OWN], BF16)
    MT = dscr("MT", [2048, OWN], BF16)
    GATE_D = dscr("GATE_D", [2, D], F32)
    scr_objs = [Obj("scr%d" % i) for i in range(1)]
    SCR = scr_objs[0]

    sy = Sy(nc)
    rr = {'q': 0}

    def dq():
        rr['q'] += 1
        return 'sp' if rr['q'] % 2 else 'act'

    with ExitStack() as glob:
        def G(name, shape, dt, space='sb'):
            if space == 'sb':
                t = glob.enter_context(nc.sbuf_tensor(_uniq(name), list(shape), dt))
            else:
                t = glob.enter_context(nc.psum_tensor(_uniq(name), list(shape), dt))
            return t, Obj(name)

        identf, IDENTF = G("identf", [128, 128], F32)
        identb, IDENTB = G("identb", [128, 128], BF16)
        ones16, ONES16 = G("ones16", [128, 128], BF16)
        onesf, ONESF = G("onesf", [128, 128], F32)
        shiftT, SHIFTT = G("shiftT", [128, KC, 2], F32)
        opsT, OPST = G("opsT", [128, KC, 2], F32)
        gate_b, GATEB = G("gate_b", [128, D], F32)
        sy.dma('sp', identf[:], ident_in, IDENTF, writes=[IDENTF])
        sy.op('dve', lambda: nc.vector.tensor_copy(out=identb[:], in_=identf[:]), reads=[IDENTF], writes=[IDENTB])
        sy.op('dve', lambda: nc.vector.memset(ones16[:], 1.0), writes=[ONES16])
        sy.op('dve', lambda: nc.vector.memset(onesf[:], 1.0), writes=[ONESF])

        with ExitStack() as ph:
            def T(name, shape, dt, space='sb'):
                if space == 'sb':
                    t = ph.enter_context(nc.sbuf_tensor(_uniq(name), list(shape), dt))
                else:
                    t = ph.enter_context(nc.psum_tensor(_uniq(name), list(shape), dt))
                return t, Obj(name)
            cv, CV = T("cv", [128, KC, 2], F32)
            sc, SC = T("sc", [128, KC, 2], BF16)
            bmt, BMT = T("bmt", [128, 48], F32)
            bmg, BMG = T("bmg", [2, D], F32)
            grow, GROW = T("grow", [2, D], F32)
            modT, MODT = T("modT", [128, 48, 2], F32)
            wm = [T("wm%d" % i, [128, KC, 512], BF16) for i in range(2)]
            pmod, PMOD = T("pmod", [128, 512], F32, 'ps')
            pg = [T("pg%d" % i, [128, 512], F32, 'ps') for i in range(2)]
            sy.dma('sp', cv[:], cvec, CV, writes=[CV])
            sy.dma('act', bmt[:], bmod_t, BMT, writes=[BMT])
            sy.dma('act', bmg[:], bmod_g, BMG, writes=[BMG])
            sy.op('act', lambda: nc.scalar.activation(out=sc[:], in_=cv[:], func=AF.Silu), reads=[CV], writes=[SC])
            wm_v = w_mod.rearrange("(k p) c -> p k c", p=128)
            for cg in range(12):
                wt, WT = wm[cg % 2]
                sy.dma('pool', wt[:], wm_v[:, :, cg * 512:(cg + 1) * 512], WT, writes=[WT])
                for ct in range(4):
                    j = cg * 4 + ct
                    for k in range(KC):
                        sy.mm(lambda: nc.tensor.matmul(pmod[:, 2 * j:2 * j + 2], lhsT=wt[:, k, ct * 128:(ct + 1) * 128],
                                                       rhs=sc[:, k, :], start=(k == 0), stop=(k == KC - 1)),
                              reads=[WT, SC], writes=[PMOD], last=(k == KC - 1))
                if cg >= 8:
                    pgt, PGT = pg[cg % 2]
                    for k in range(KC):
                        sy.mm(lambda: nc.tensor.matmul(pgt[0:2, :], lhsT=sc[:, k, :], rhs=wt[:, k, :], start=(k == 0), stop=(k == KC - 1)),
                              reads=[WT, SC], writes=[PGT], last=(k == KC - 1))
                    c0 = (cg - 8) * 512
                    sy.op('dve', lambda: nc.vector.tensor_tensor(out=grow[:, c0:c0 + 512], in0=pgt[0:2, :], in1=bmg[:, c0:c0 + 512], op=ALU.add),
                          reads=[PGT, BMG], writes=[GROW])
            sy.op('dve', lambda: nc.vector.tensor_tensor(out=modT[:], in0=pmod[:, 0:96].rearrange("p (j r) -> p j r", r=2),
                                                         in1=bmt[:].unsqueeze(2).to_broadcast([128, 48, 2]), op=ALU.add),
                  reads=[PMOD, BMT], writes=[MODT])
            sy.op('dve', lambda: nc.vector.tensor_copy(out=shiftT[:], in_=modT[:, 0:16, :]), reads=[MODT], writes=[SHIFTT])
            sy.op('dve', lambda: nc.vector.tensor_scalar(out=opsT[:], in0=modT[:, 16:32, :], scalar1=1.0, scalar2=None, op0=ALU.add),
                  reads=[MODT], writes=[OPST])
            sy.dma('sp', GATE_D, grow[:], GROW, reads=[GROW], writes=[SCR])
            sy.barrier([SCR])
            sy.dma('sp', gate_b[:], GATE_D[0:1, :].partition_broadcast(128), GATEB, reads=[SCR], writes=[GATEB])
            sy.barrier([SCR, GATEB])

        with ExitStack() as ph:
            def T(name, shape, dt, space='sb'):
                if space == 'sb':
                    t = ph.enter_context(nc.sbuf_tensor(_uniq(name), list(shape), dt))
                else:
                    t = ph.enter_context(nc.psum_tensor(_uniq(name), list(shape), dt))
                return t, Obj(name)
            xts = [T("xt%d" % i, [128, D], F32) for i in range(3)]
            xhs = [T("xh%d" % i, [128, 4, D], BF16) for i in range(2)]
            uts = [T("ut%d" % i, [128, KC, 512], BF16) for i in range(2)]
            sts = [T("st%d" % i, [128, 4, 6], F32) for i in range(2)]
            mvs = [T("mv%d" % i, [128, 4], F32) for i in range(2)]
            pts = [T("pt%d" % i, [128, 512], BF16, 'ps') for i in range(3)]
            groups = _groups(0, NLAT) + _groups(NLAT, NT)
            xi = 0
            pi = 0
            for gi, (t0, n) in enumerate(groups):
                isctx = 1 if t0 >= NLAT else 0
                ntile = n // 128
                xh, XH = xhs[gi % 2]
                ut, UTO = uts[gi % 2]
                for tt in range(ntile):
                    xt, XT = xts[xi % 3]
                    st, ST = sts[xi % 2]
                    mv, MV = mvs[xi % 2]
                    xi += 1
                    r0 = t0 + tt * 128
                    sy.dma(dq(), xt[:], xa[r0:r0 + 128, :], XT, writes=[XT])
                    for j in range(4):
                        sy.op('dve', lambda: nc.vector.bn_stats(out=st[:, j, :], in_=xt[:, j * 512:(j + 1) * 512]), reads=[XT], writes=[ST])
                    sy.op('dve', lambda: nc.vector.bn_aggr(out=mv[:, 0:2], in_=st[:].rearrange("p a b -> p (a b)")), reads=[ST], writes=[MV])
                    sy.op('dve', lambda: nc.vector.tensor_scalar(out=mv[:, 2:3], in0=mv[:, 1:2], scalar1=EPS, scalar2=None, op0=ALU.add),
                          reads=[MV], writes=[MV])
                    sy.op('act', lambda: nc.scalar.activation(out=mv[:, 2:3], in_=mv[:, 2:3], func=AF.Ln), reads=[MV], writes=[MV])
                    sy.op('act', lambda: nc.scalar.activation(out=mv[:, 2:3], in_=mv[:, 2:3], func=AF.Exp, scale=-0.5), reads=[MV], writes=[MV])
                    sy.op('dve', lambda: nc.vector.scalar_tensor_tensor(out=mv[:, 3:4], in0=mv[:, 0:1], scalar=-1.0, in1=mv[:, 2:3], op0=ALU.mult, op1=ALU.mult),
                          reads=[MV], writes=[MV])
                    sy.op('act', lambda: nc.scalar.activation(out=xh[:, tt, :], in_=xt[:], func=AF.Identity, scale=mv[:, 2:3], bias=mv[:, 3:4]),
                          reads=[XT, MV], writes=[XH])
                for k in range(KC):
                    pt, PT = pts[pi % 3]
                    pi += 1
                    for tt in range(ntile):
                        sy.mm(lambda: nc.tensor.transpose(out=pt[:, tt * 128:(tt + 1) * 128], in_=xh[:, tt, k * 128:(k + 1) * 128], identity=identb[:]),
                              reads=[XH, IDENTB], writes=[PT], last=(tt == ntile - 1))
                    if k % 2 == 0:
                        sy.op('dve', lambda: nc.vector.tensor_scalar(out=ut[:, k, 0:n], in0=pt[:, 0:n], scalar1=opsT[:, k, isctx:isctx + 1],
                                                                     scalar2=shiftT[:, k, isctx:isctx + 1], op0=ALU.mult, op1=ALU.add),
                              reads=[PT, OPST, SHIFTT], writes=[UTO])
                    else:
                        sy.op('act', lambda: nc.scalar.activation(out=ut[:, k, 0:n], in_=pt[:, 0:n], func=AF.Identity,
                                                                  scale=opsT[:, k, isctx:isctx + 1], bias=shiftT[:, k, isctx:isctx + 1]),
                              reads=[PT, OPST, SHIFTT], writes=[UTO])
                sy.dma('sp', UT[t0 // 512].rearrange("p (k t) -> p k t", k=KC)[:, :, 0:n], ut[:, :, 0:n], UTO, reads=[UTO], writes=[SCR])
            sy.barrier([SCR] + [o for (_, o) in xts + xhs + uts])

        with ExitStack() as ph:
            def T(name, shape, dt, space='sb'):
                if space == 'sb':
                    t = ph.enter_context(nc.sbuf_tensor(_uniq(name), list(shape), dt))
                else:
                    t = ph.enter_context(nc.psum_tensor(_uniq(name), list(shape), dt))
                return t, Obj(name)
            wbs = [T("wb%d" % i, [128, KC, 1024], BF16) for i in range(2)]
            utl = [T("utl%d" % i, [128, KC, 512], BF16) for i in range(2)]
            stg = [T("stg%d" % i, [128, 8, 512], BF16) for i in range(2)]
            stk = [T("stk%d" % i, [128, 1024], BF16) for i in range(2)]
            stgf, STGF = T("stgf", [128, 32], F32)
            pps = [T("pp%d" % i, [128, 512], F32, 'ps') for i in range(4)]
            ptr = [T("ptr%d" % i, [128, 512], BF16, 'ps') for i in range(2)]
            nrt = [dict(xs=T("xs%d" % i, [128, 512], F32), sq=T("sq%d" % i, [128, 512], F32), t1=T("t1%d" % i, [128, 512], F32),
                        t2=T("t2%d" % i, [128, 512], F32), ss=T("ss%d" % i, [128, 8], F32), ybf=T("ybf%d" % i, [128, 512], BF16)) for i in range(2)]
            cosb = [T("cosb%d" % i, [128, 128], F32) for i in range(2)]
            sinb = [T("sinb%d" % i, [128, 128], F32) for i in range(2)]
            nwq, NWQ = T("nwq", [128, 128], F32)
            nwk, NWK = T("nwk", [128, 128], F32)
            stT = [T("stT%d" % i, [128, 4, 512], BF16) for i in range(2)]
            sy.dma('sp', nwq[:], qnw.partition_broadcast(128), NWQ, writes=[NWQ])
            sy.dma('sp', nwk[:], knw.partition_broadcast(128), NWK, writes=[NWK])
            w_v = w_in.rearrange("(k p) c -> p k c", p=128)
            cnt = {'w': 0, 'u': 0, 'p': 0, 's': 0, 'k': 0, 'tr': 0, 'cs': 0, 'st': 0, 'nr': 0}

            def load_w(c0, ncols):
                wb, WB = wbs[cnt['w'] % 2]
                cnt['w'] += 1
                sy.dma('pool', wb[:, :, 0:ncols], w_v[:, :, c0:c0 + ncols], WB, writes=[WB])
                return wb, WB

            def load_u(t0, n):
                u, U = utl[cnt['u'] % 2]
                cnt['u'] += 1
                sy.dma('sp', u[:, :, 0:n], UT[t0 // 512].rearrange("p (k t) -> p k t", k=KC)[:, :, 0:n], U, reads=[SCR], writes=[U])
                return u, U

            def nextp():
                p = pps[cnt['p'] % 4]
                cnt['p'] += 1
                return p

            def block_fm(c0, ncols, tgroups, func, dst, drow0):
                wb, WB = load_w(c0, ncols)
                nct = ncols // 128
                dst_v = dst[drow0:drow0 + ncols, :].rearrange("(c p) t -> p c t", p=128)
                for (t0, n) in tgroups:
                    u, U = load_u(t0, n)
                    sg, SGO = stg[cnt['s'] % 2]
                    cnt['s'] += 1
                    for ct in range(nct):
                        pp, PP = nextp()
                        for k in range(KC):
                            sy.mm(lambda: nc.tensor.matmul(pp[:, 0:n], lhsT=wb[:, k, ct * 128:(ct + 1) * 128], rhs=u[:, k, 0:n],
                                                           start=(k == 0), stop=(k == KC - 1)),
                                  reads=[WB, U], writes=[PP], last=(k == KC - 1))
                        if func is None:
                            if ct % 2 == 0:
                                sy.op('dve', lambda: nc.vector.tensor_copy(out=sg[:, ct, 0:n], in_=pp[:, 0:n]), reads=[PP], writes=[SGO])
                            else:
                                sy.op('act', lambda: nc.scalar.copy(out=sg[:, ct, 0:n], in_=pp[:, 0:n]), reads=[PP], writes=[SGO])
                        else:
                            sy.op('act', lambda: nc.scalar.activation(out=sg[:, ct, 0:n], in_=pp[:, 0:n], func=func), reads=[PP], writes=[SGO])
                    sy.dma('act', dst_v[:, :, t0:t0 + n], sg[:, 0:nct, 0:n], SGO, reads=[SGO], writes=[SCR])

            def block_tm(c0, ncols, tgroups, dst, dcol0):
                wb, WB = load_w(c0, ncols)
                ncg = (ncols + 511) // 512
                for (t0, n) in tgroups:
                    u, U = load_u(t0, n)
                    for tt in range(n // 128):
                        sk, SKO = stk[cnt['k'] % 2]
                        cnt['k'] += 1
                        for cg in range(ncg):
                            w = min(512, ncols - cg * 512)
                            pp, PP = nextp()
                            for k in range(KC):
                                sy.mm(lambda: nc.tensor.matmul(pp[:, 0:w], lhsT=u[:, k, tt * 128:(tt + 1) * 128], rhs=wb[:, k, cg * 512:cg * 512 + w],
                                                               start=(k == 0), stop=(k == KC - 1)),
                                      reads=[WB, U], writes=[PP], last=(k == KC - 1))
                            if cg % 2 == 0:
                                sy.op('dve', lambda: nc.vector.tensor_copy(out=sk[:, cg * 512:cg * 512 + w], in_=pp[:, 0:w]), reads=[PP], writes=[SKO])
                            else:
                                sy.op('act', lambda: nc.scalar.copy(out=sk[:, cg * 512:cg * 512 + w], in_=pp[:, 0:w]), reads=[PP], writes=[SKO])
                        r0 = t0 + tt * 128
                        sy.dma('act', dst[r0:r0 + 128, dcol0:dcol0 + ncols], sk[:, 0:ncols], SKO, reads=[SKO], writes=[SCR])

            def block_gates(tgroups):
                wb, WB = load_w(C_IF, 32)
                for (t0, n) in tgroups:
                    u, U = load_u(t0, n)
                    for tt in range(n // 128):
                        pp, PP = nextp()
                        for k in range(KC):
                            sy.mm(lambda: nc.tensor.matmul(pp[:, 0:32], lhsT=u[:, k, tt * 128:(tt + 1) * 128], rhs=wb[:, k, 0:32],
                                                           start=(k == 0), stop=(k == KC - 1)),
                                  reads=[WB, U], writes=[PP], last=(k == KC - 1))
                        sy.op('dve', lambda: nc.vector.tensor_copy(out=stgf[:], in_=pp[:, 0:32]), reads=[PP], writes=[STGF])
                        r0 = t0 + tt * 128
                        sy.dma('act', GT[r0:r0 + 128, :], stgf[:], STGF, reads=[STGF], writes=[SCR])

            def block_nr(c0, ncols, tgroups, nw, NW, dst, dhead0):
                wb, WB = load_w(c0, ncols)
                ncg = ncols // 512
                for (t0, n) in tgroups:
                    u, U = load_u(t0, n)
                    ntile = n // 128
                    for cg in range(ncg):
                        sT, STO = stT[cnt['st'] % 2]
                        cnt['st'] += 1
                        for tt in range(ntile):
                            cb, CB = cosb[cnt['cs'] % 2]
                            sb_, SB = sinb[cnt['cs'] % 2]
                            cnt['cs'] += 1
                            r0 = t0 + tt * 128
                            sy.dma('sp', cb[:], rope_c[r0:r0 + 128, :], CB, writes=[CB])
                            sy.dma('sp', sb_[:], rope_s[r0:r0 + 128, :], SB, writes=[SB])
                            pp, PP = nextp()
                            _n = nrt[cnt['nr'] % 2]
                            cnt['nr'] += 1
                            xs_, XS = _n['xs']
                            sq_, SQ = _n['sq']
                            t1_, T1 = _n['t1']
                            t2_, T2 = _n['t2']
                            ss_, SS = _n['ss']
                            ybf, YBF = _n['ybf']
                            for k in range(KC):
                                sy.mm(lambda: nc.tensor.matmul(pp[:], lhsT=u[:, k, tt * 128:(tt + 1) * 128], rhs=wb[:, k, cg * 512:(cg + 1) * 512],
                                                               start=(k == 0), stop=(k == KC - 1)),
                                      reads=[WB, U], writes=[PP], last=(k == KC - 1))
                            sy.op('act', lambda: nc.scalar.copy(out=xs_[:], in_=pp[:]), reads=[PP], writes=[XS])
                            sy.op('pool', lambda: nc.gpsimd.tensor_tensor(out=sq_[:], in0=xs_[:], in1=xs_[:], op=ALU.mult), reads=[XS], writes=[SQ])
                            sy.op('dve', lambda: nc.vector.tensor_reduce(out=ss_[:, 0:4], in_=sq_[:].rearrange("p (h d) -> p h d", h=4), axis=AX.X, op=ALU.add),
                                  reads=[SQ], writes=[SS])
                            sy.op('dve', lambda: nc.vector.tensor_scalar(out=ss_[:, 4:8], in0=ss_[:, 0:4], scalar1=1.0 / 128.0, scalar2=EPS, op0=ALU.mult, op1=ALU.add),
                                  reads=[SS], writes=[SS])
                            sy.op('act', lambda: nc.scalar.activation(out=ss_[:, 4:8], in_=ss_[:, 4:8], func=AF.Ln), reads=[SS], writes=[SS])
                            sy.op('act', lambda: nc.scalar.activation(out=ss_[:, 0:4], in_=ss_[:, 4:8], func=AF.Exp, scale=-0.5), reads=[SS], writes=[SS])
                            x3 = xs_[:].rearrange("p (h d) -> p h d", h=4)
                            sy.op('dve', lambda: nc.vector.tensor_tensor(out=t1_[:].rearrange("p (h d) -> p h d", h=4), in0=x3,
                                                                         in1=ss_[:, 0:4].unsqueeze(2).to_broadcast([128, 4, 128]), op=ALU.mult),
                                  reads=[XS, SS], writes=[T1])
                            sy.op('pool', lambda: nc.gpsimd.tensor_tensor(out=xs_[:].rearrange("p (h d) -> p h d", h=4), in0=t1_[:].rearrange("p (h d) -> p h d", h=4),
                                                                          in1=nw[:].unsqueeze(1).to_broadcast([128, 4, 128]), op=ALU.mult),
                                  reads=[T1, NW], writes=[XS])
                            sy.op('dve', lambda: nc.vector.tensor_tensor(out=t1_[:].rearrange("p (h d) -> p h d", h=4), in0=x3,
                                                                         in1=cb[:].unsqueeze(1).to_broadcast([128, 4, 128]), op=ALU.mult),
                                  reads=[XS, CB], writes=[T1])
                            x5 = xs_[:].rearrange("p (h a b c) -> p h a b c", h=4, a=2, b=2)
                            t5 = t2_[:].rearrange("p (h a b c) -> p h a b c", h=4, a=2, b=2)
                            s5 = sb_[:].rearrange("p (a b c) -> p a b c", a=2, b=2)
                            for hh in range(2):
                                for bb in range(2):
                                    sy.op('pool', lambda: nc.gpsimd.tensor_tensor(out=t5[:, :, hh, bb, :], in0=x5[:, :, hh, 1 - bb, :],
                                                                                  in1=s5[:, hh, bb, :].unsqueeze(1).to_broadcast([128, 4, 32]), op=ALU.mult),
                                          reads=[XS, SB], writes=[T2])
                            sy.op('dve', lambda: nc.vector.tensor_tensor(out=ybf[:], in0=t1_[:], in1=t2_[:], op=ALU.add), reads=[T1, T2], writes=[YBF])
                            ptt, PTT = ptr[cnt['tr'] % 2]
                            cnt['tr'] += 1
                            for h in range(4):
                                sy.mm(lambda: nc.tensor.transpose(out=ptt[:, h * 128:(h + 1) * 128], in_=ybf[:, h * 128:(h + 1) * 128], identity=identb[:]),
                                      reads=[YBF, IDENTB], writes=[PTT], last=(h == 3))
                            sy.op('act', lambda: nc.scalar.copy(out=sT[:, :, tt * 128:(tt + 1) * 128], in_=ptt[:].rearrange("p (h t) -> p h t", h=4)),
                                  reads=[PTT], writes=[STO])
                        h0 = dhead0 + cg * 4
                        dv = dst[h0 * 128:(h0 + 4) * 128, :].rearrange("(h p) t -> p h t", p=128)
                        sy.dma('act', dv[:, :, t0:t0 + n], sT[:, :, 0:n], STO, reads=[STO], writes=[SCR])

            own_g = _groups(0, OWN)
            all_g = _groups(0, NLAT) + _groups(NLAT, NT)
            qa_g = _groups(0, min(OWN + 512, NLAT))
            block_fm(C_QA, 1024, qa_g, None, QKT, 0)
            block_fm(C_KA, 1024, all_g, None, QKT, 1024)
            block_tm(C_VA, 1024, all_g, VA, 0)
            block_tm(C_VA + 1024, 1024, all_g, VA, 1024)
            block_tm(C_VB, 512, all_g, VB, 0)
            block_gates(all_g)
            block_nr(C_KB, 512, all_g, nwk, NWK, KBT, 0)
            block_nr(C_QB, 1024, own_g, nwq, NWQ, QBT, 0)
            block_nr(C_QB + 1024, 1024, own_g, nwq, NWQ, QBT, 8)
            for i, (c0, fn) in enumerate([(C_OA, AF.Sigmoid), (C_ZA, AF.Silu), (C_ZB, AF.Silu), (C_GA, AF.Sigmoid), (C_GB, AF.Sigmoid)]):
                block_fm(c0, 1024, own_g, fn, SG, i * 2048)
                block_fm(c0 + 1024, 1024, own_g, fn, SG, i * 2048 + 1024)
            sy.barrier([SCR] + [o for (_, o) in wbs + utl + stg + stk + stT] + [STGF])

        with ExitStack() as ph:
            def T(name, shape, dt, space='sb'):
                if space == 'sb':
                    t = ph.enter_context(nc.sbuf_tensor(_uniq(name), list(shape), dt))
                else:
                    t = ph.enter_context(nc.psum_tensor(_uniq(name), list(shape), dt))
                return t, Obj(name)
            NG = NCH * 8
            tri, TRI = T("tri", [64, 4, 64], F32)
            mskb, MSKB = T("mskb", [64, 2, 64], BF16)
            cw, CW = T("cw", [128, KC, 3], F32)
            cbt, CBT = T("cbt", [128, KC], F32)
            Ee = [T("Ee%d" % d, [64, NG], F32) for d in range(2)]
            Fl = [T("Fl%d" % d, [64, NG], F32) for d in range(2)]
            RBA = [T("RBA%d" % d, [128, NG], F32) for d in range(2)]
            pbig, PBIG = T("pbig", [128, 2048], F32, 'ps')
            ptk_, PTK = T("ptk", [128, 1024], BF16, 'ps')
            ptk = ptk_[0:64, :]
            pst_, PST = T("pst", [128, 512], F32, 'ps')
            pst = pst_[0:64, :]
            pden_, PDEN = T("pden", [128, 512], F32, 'ps')
            pden = pden_[0:64, 0:8]
            pdn_, PDN = T("pdn", [128, 512], F32, 'ps')
            pdn = pdn_[:, 0:8]
            pre_es = ExitStack()

            def T2(name, shape, dt):
                t = pre_es.enter_context(nc.sbuf_tensor(_uniq(name), list(shape), dt))
                return t, Obj(name)
            bif, BIF = T2("bif", [64, 32], F32)
            gts, GTS = T2("gts", [64, NCH, 32], F32)
            _lf = T2("LF", [64, NG], F32)
            _bc = T2("Bc", [64, NG], F32)
            _gm = T2("Gm", [64, NG], F32)
            _rw = T2("rows", [1, 5, NG], F32)
            _rbm = T2("RBM", [128, NG], F32)
            LF = [_lf, _lf]
            Bc = [_bc, _bc]
            Gm = [_gm, _gm]
            rows = [_rw, _rw]
            RBM = [_rbm, _rbm]
            mrun, MRUN = T2("mrun", [1, 8], F32)
            tmpc, TMPC = T2("tmpc", [128, 1], F32)
            sy.dma('sp', tri[:], tri_in.rearrange("a t s -> t a s"), TRI, writes=[TRI])
            sy.dma('sp', bif[:], b_if.partition_broadcast(64), BIF, writes=[BIF])
            sy.dma('sp', cw[:], convw_t, CW, writes=[CW])
            sy.dma('sp', cbt[:], convb_t, CBT, writes=[CBT])
            sy.op('dve', lambda: nc.vector.tensor_copy(out=mskb[:], in_=tri[:, 2:4, :]), reads=[TRI], writes=[MSKB])
            GT_v = GT.rearrange("(c p) g -> p c g", p=64)
            for c0 in range(0, NCH, 16):
                c1 = min(NCH, c0 + 16)
                sy.dma('sp', gts[:, c0:c1, :], GT_v[:, c0:c1, :], GTS, reads=[SCR], writes=[GTS])
            sy.op('dve', lambda: nc.vector.tensor_tensor(out=gts[:], in0=gts[:], in1=bif[:].unsqueeze(1).to_broadcast([64, NCH, 32]), op=ALU.add),
                  reads=[GTS, BIF], writes=[GTS])
            nown = OWN // 64
            noth = OTH // 64
            nctx = CTX // 64
            seqs = [list(range(nown + noth, NCH)) + list(range(0, nown)),
                    list(range(NCH - 1, nown + noth - 1, -1)) + list(range(nown + noth - 1, -1, -1))]
            for d in range(2):
                lf, LFO = LF[d]
                bc, BCO = Bc[d]
                gm, GMO = Gm[d]
                ee, EEO = Ee[d]
                fl, FLO = Fl[d]
                rw, RWO = rows[d]
                rbm, RBMO = RBM[d]
                rba, RBAO = RBA[d]
                gi_v = gts[:, :, 16 * d:16 * d + 8]
                gf_v = gts[:, :, 16 * d + 8:16 * d + 16]
                lf3 = lf[:].rearrange("p (c h) -> p c h", h=8)
                sy.op('act', lambda: nc.scalar.activation(out=lf3, in_=gf_v, func=AF.Exp, scale=-1.0), reads=[GTS], writes=[LFO])
                sy.op('dve', lambda: nc.vector.tensor_scalar(out=lf[:], in0=lf[:], scalar1=1.0, scalar2=None, op0=ALU.add), reads=[LFO], writes=[LFO])
                sy.op('act', lambda: nc.scalar.activation(out=lf[:], in_=lf[:], func=AF.Ln), reads=[LFO], writes=[LFO])
                sy.op('dve', lambda: nc.vector.tensor_scalar(out=lf[:], in0=lf[:], scalar1=-1.0, scalar2=None, op0=ALU.mult), reads=[LFO], writes=[LFO])
                for n0 in range(0, NG, 512):
                    w = min(512, NG - n0)
                    sy.mm(lambda: nc.tensor.matmul(pbig[0:64, 0:w], lhsT=tri[:, d, :], rhs=lf[:, n0:n0 + w], start=True, stop=True),
                          reads=[TRI, LFO], writes=[PBIG])
                    sy.op('dve', lambda: nc.vector.tensor_copy(out=bc[:, n0:n0 + w], in_=pbig[0:64, 0:w]), reads=[PBIG], writes=[BCO])
                    sy.mm(lambda: nc.tensor.matmul(pbig[0:1, 512:512 + w], lhsT=onesf[0:64, 0:1], rhs=lf[:, n0:n0 + w], start=True, stop=True),
                          reads=[ONESF, LFO], writes=[PBIG])
                    sy.op('dve', lambda: nc.vector.tensor_copy(out=rw[:, 1, n0:n0 + w], in_=pbig[0:1, 512:512 + w]), reads=[PBIG], writes=[RWO])
                sy.op('dve', lambda: nc.vector.tensor_tensor(out=gm[:].rearrange("p (c h) -> p c h", h=8), in0=gi_v,
                                                             in1=bc[:].rearrange("p (c h) -> p c h", h=8), op=ALU.subtract),
                      reads=[GTS, BCO], writes=[GMO])
                for n0 in range(0, NG, 128):
                    w = min(128, NG - n0)
                    sy.mm(lambda: nc.tensor.transpose(out=pbig[0:w, 1024:1088], in_=gm[:, n0:n0 + w], identity=identf[0:64, 0:64]),
                          reads=[GMO, IDENTF], writes=[PBIG])
                    sy.op('dve', lambda: nc.vector.tensor_reduce(out=tmpc[0:w, 0:1], in_=pbig[0:w, 1024:1088], axis=AX.X, op=ALU.max),
                          reads=[PBIG], writes=[TMPC])
                    sy.mm(lambda: nc.tensor.transpose(out=pbig[0:1, 1536:1536 + w], in_=tmpc[0:w, 0:1], identity=identf[0:w, 0:w]),
                          reads=[TMPC, IDENTF], writes=[PBIG])
                    sy.op('dve', lambda: nc.vector.tensor_copy(out=rw[:, 0, n0:n0 + w], in_=pbig[0:1, 1536:1536 + w]), reads=[PBIG], writes=[RWO])
                sy.op('dve', lambda: nc.vector.memset(mrun[:], -1e30), writes=[MRUN])
                for c in seqs[d]:
                    sl = slice(c * 8, c * 8 + 8)
                    sy.op('dve', lambda: nc.vector.tensor_copy(out=rw[:, 3, sl], in_=mrun[:]), reads=[MRUN], writes=[RWO])
                    sy.op('dve', lambda: nc.vector.tensor_tensor(out=rw[:, 2, sl], in0=mrun[:], in1=rw[:, 0, sl], op=ALU.max), reads=[MRUN, RWO], writes=[RWO])
                    sy.op('dve', lambda: nc.vector.tensor_tensor(out=mrun[:], in0=rw[:, 2, sl], in1=rw[:, 1, sl], op=ALU.add), reads=[RWO], writes=[MRUN])
                sy.op('dve', lambda: nc.vector.tensor_tensor(out=rw[:, 4, :], in0=rw[:, 3, :], in1=rw[:, 2, :], op=ALU.subtract), reads=[RWO], writes=[RWO])
                sy.op('dve', lambda: nc.vector.tensor_scalar(out=rw[:, 4, :], in0=rw[:, 4, :], scalar1=-80.0, scalar2=None, op0=ALU.max), reads=[RWO], writes=[RWO])
                sy.op('act', lambda: nc.scalar.activation(out=rw[:, 4, :], in_=rw[:, 4, :], func=AF.Exp), reads=[RWO], writes=[RWO])
                for n0 in range(0, NG, 512):
                    w = min(512, NG - n0)
                    sy.mm(lambda: nc.tensor.matmul(pbig[:, 0:w], lhsT=onesf[0:1, :], rhs=rw[:, 2, n0:n0 + w], start=True, stop=True),
                          reads=[ONESF, RWO], writes=[PBIG])
                    sy.op('dve', lambda: nc.vector.tensor_copy(out=rbm[:, n0:n0 + w], in_=pbig[:, 0:w]), reads=[PBIG], writes=[RBMO])
                    sy.mm(lambda: nc.tensor.matmul(pbig[:, 512:512 + w], lhsT=onesf[0:1, :], rhs=rw[:, 4, n0:n0 + w], start=True, stop=True),
                          reads=[ONESF, RWO], writes=[PBIG])
                    sy.op('dve', lambda: nc.vector.tensor_copy(out=rba[:, n0:n0 + w], in_=pbig[:, 512:512 + w]), reads=[PBIG], writes=[RBAO])
                sy.op('dve', lambda: nc.vector.tensor_tensor(out=ee[:], in0=gm[:], in1=rbm[0:64, :], op=ALU.subtract), reads=[GMO, RBMO], writes=[EEO])
                sy.op('act', lambda: nc.scalar.activation(out=ee[:], in_=ee[:], func=AF.Exp), reads=[EEO], writes=[EEO])
                sy.op('dve', lambda: nc.vector.tensor_scalar(out=ee[:], in0=ee[:], scalar1=float(DK_A ** -0.5), scalar2=None, op0=ALU.mult), reads=[EEO], writes=[EEO])
                sy.op('dve', lambda: nc.vector.tensor_tensor(out=fl[:], in0=bc[:], in1=rbm[0:64, :], op=ALU.add), reads=[BCO, RBMO], writes=[FLO])
                sy.op('act', lambda: nc.scalar.activation(out=fl[:], in_=fl[:], func=AF.Exp, scale=-1.0), reads=[FLO], writes=[FLO])

            sy.barrier([])
            pre_es.close()
            C32, C32O = T("C32", [128, 8, 256], F32)
            N32, N32O = T("N32", [128, 8], F32)
            C16, C16O = T("C16", [128, 8, 256], BF16)
            N16, N16O = T("N16", [128, 8], BF16)
            pre = [T("pre%d" % i, [128, 8, 514], BF16) for i in range(2)]
            cva, CVA = T("cva", [128, 8, 512], F32)
            cvb, CVB = T("cvb", [128, 8, 512], F32)
            kT_, KTO = T("kT", [128, 8, 512], BF16)
            qT_, QTO = T("qT", [128, 8, 512], BF16)
            vs = [T("v16_%d" % i, [64, 8, 256], BF16) for i in range(2)]
            ke_, KEO = T("ke", [64, 8, 128], BF16)
            sw1, SW1 = T("sw1", [64, 8, 64], F32)
            sw_, SWO = T("sw", [64, 8, 64], BF16)
            dd_, DDO = T("dd", [64, 16], F32)
            hh_ = [T("hh%d" % i, [64, 8, 256], F32) for i in range(1)]
            hf_ = [T("hf%d" % i, [64, 8, 256], F32) for i in range(1)]
            hq_, HQO = T("hq", [64, 8, 256], F32)
            hn_ = [T("hn%d" % i, [64, 8, 256], BF16) for i in range(2)]
            ssm, SSM = T("ssm", [64, 16], F32)
            QKT_v = QKT.rearrange("(h p) t -> p h t", p=128)
            cwb = lambda j, half: cw[:, half * 8:(half + 1) * 8, j:j + 1].to_broadcast([128, 8, 512])

            def conv_load(half, t0, n, dstT, DSTO, slot):
                pr, PRO = pre[slot]
                seq0 = 0 if t0 < NLAT else NLAT
                seq1 = NLAT if t0 < NLAT else NT
                lo = max(t0 - 1, seq0)
                hi = min(t0 + n + 1, seq1)
                if lo > t0 - 1:
                    sy.op('pool', lambda: nc.gpsimd.memset(pr[:, :, 0:1], 0.0), writes=[PRO])
                if hi < t0 + n + 1:
                    sy.op('pool', lambda: nc.gpsimd.memset(pr[:, :, n + 1:n + 2], 0.0), writes=[PRO])
                sy.dma('sp', pr[:, :, lo - (t0 - 1):hi - (t0 - 1)], QKT_v[:, half * 8:(half + 1) * 8, lo:hi], PRO, reads=[SCR], writes=[PRO])
                sy.op('pool', lambda: nc.gpsimd.tensor_tensor(out=cva[:, :, 0:n], in0=pr[:, :, 0:n], in1=cwb(0, half)[:, :, 0:n], op=ALU.mult), reads=[PRO, CW], writes=[CVA])
                sy.op('pool', lambda: nc.gpsimd.tensor_tensor(out=cvb[:, :, 0:n], in0=pr[:, :, 1:n + 1], in1=cwb(1, half)[:, :, 0:n], op=ALU.mult), reads=[PRO, CW], writes=[CVB])
                sy.op('pool', lambda: nc.gpsimd.tensor_tensor(out=cva[:, :, 0:n], in0=cva[:, :, 0:n], in1=cvb[:, :, 0:n], op=ALU.add), reads=[CVA, CVB], writes=[CVA])
                sy.op('pool', lambda: nc.gpsimd.tensor_tensor(out=cvb[:, :, 0:n], in0=pr[:, :, 2:n + 2], in1=cwb(2, half)[:, :, 0:n], op=ALU.mult), reads=[PRO, CW], writes=[CVB])
                sy.op('pool', lambda: nc.gpsimd.tensor_tensor(out=cva[:, :, 0:n], in0=cva[:, :, 0:n], in1=cvb[:, :, 0:n], op=ALU.add), reads=[CVA, CVB], writes=[CVA])
                for h in range(8):
                    sy.op('act', lambda: nc.scalar.activation(out=dstT[:, h, 0:n], in_=cva[:, h, 0:n], func=AF.Silu, bias=cbt[:, half * 8 + h:half * 8 + h + 1]),
                          reads=[CVA, CBT], writes=[DSTO])

            vi = 0
            hi_ = 0
            for d in range(2):
                ee, EEO = Ee[d]
                fl, FLO = Fl[d]
                rba, RBAO = RBA[d]
                ee3 = ee[:].rearrange("p (c h) -> p c h", h=8)
                fl3 = fl[:].rearrange("p (c h) -> p c h", h=8)
                rba3 = rba[:].rearrange("p (c h) -> p c h", h=8)
                sy.op('dve', lambda: nc.vector.memset(C32[:], 0.0), writes=[C32O])
                sy.op('dve', lambda: nc.vector.memset(N32[:], 0.0), writes=[N32O])
                cur_sc = None
                for c in seqs[d]:
                    is_out = c < nown
                    tok0 = c * 64
                    if tok0 < NLAT:
                        s0 = (tok0 // 512) * 512
                        sn = 512
                    else:
                        s0 = NLAT
                        sn = CTX
                    if cur_sc != s0:
                        cur_sc = s0
                        conv_load(1, s0, sn, kT_, KTO, 0)
                        if is_out:
                            conv_load(0, s0, sn, qT_, QTO, 1)
                    j0 = tok0 - s0
                    v16, V16O = vs[vi % 2]
                    vi += 1
                    sy.dma('act', v16[:], VA[tok0:tok0 + 64, :].rearrange("p (h v) -> p h v", h=8), V16O, reads=[SCR], writes=[V16O])
                    for h in range(8):
                        sy.mm(lambda: nc.tensor.transpose(out=ptk[:, h * 128:(h + 1) * 128], in_=kT_[:, h, j0:j0 + 64], identity=identb[:]),
                              reads=[KTO, IDENTB], writes=[PTK], last=(h == 7))
                    sy.op('dve', lambda: nc.vector.tensor_tensor(out=ke_[:], in0=ptk.rearrange("p (h d) -> p h d", h=8),
                                                                 in1=ee3[:, c, :].unsqueeze(2).to_broadcast([64, 8, 128]), op=ALU.mult),
                          reads=[PTK, EEO], writes=[KEO])
                    sy.op('dve', lambda: nc.vector.tensor_tensor(out=C32[:], in0=C32[:], in1=rba3[:, c, :].unsqueeze(2).to_broadcast([128, 8, 256]), op=ALU.mult),
                          reads=[C32O, RBAO], writes=[C32O])
                    sy.op('dve', lambda: nc.vector.tensor_tensor(out=N32[:], in0=N32[:], in1=rba3[:, c, :], op=ALU.mult), reads=[N32O, RBAO], writes=[N32O])
                    if is_out:
                        sy.op('act', lambda: nc.scalar.copy(out=C16[:], in_=C32[:]), reads=[C32O], writes=[C16O])
                        sy.op('act', lambda: nc.scalar.copy(out=N16[:], in_=N32[:]), reads=[N32O], writes=[N16O])
                        for h in range(8):
                            sy.mm(lambda: nc.tensor.matmul(pst[:, h * 64:(h + 1) * 64], lhsT=kT_[:, h, j0:j0 + 64], rhs=qT_[:, h, j0:j0 + 64], start=True, stop=True),
                                  reads=[KTO, QTO], writes=[PST], last=(h == 7))
                        sy.op('dve', lambda: nc.vector.tensor_tensor(out=sw1[:], in0=pst.rearrange("p (h j) -> p h j", h=8),
                                                                     in1=ee3[:, c, :].unsqueeze(2).to_broadcast([64, 8, 64]), op=ALU.mult),
                              reads=[PST, EEO], writes=[SW1])
                        sy.op('pool', lambda: nc.gpsimd.tensor_tensor(out=sw_[:], in0=sw1[:], in1=mskb[:, d, :].unsqueeze(1).to_broadcast([64, 8, 64]), op=ALU.mult),
                              reads=[SW1, MSKB], writes=[SWO])
                        for h in range(8):
                            sy.mm(lambda: nc.tensor.matmul(pbig[0:64, h * 256:(h + 1) * 256], lhsT=qT_[:, h, j0:j0 + 64], rhs=C16[:, h, :], start=True, stop=False),
                                  reads=[QTO, C16O], writes=[PBIG], last=False)
                            sy.mm(lambda: nc.tensor.matmul(pbig[0:64, h * 256:(h + 1) * 256], lhsT=sw_[:, h, :], rhs=v16[:, h, :], start=False, stop=True),
                                  reads=[SWO, V16O], writes=[PBIG], last=False)
                        for h in range(8):
                            sy.mm(lambda: nc.tensor.matmul(pden[:, h:h + 1], lhsT=qT_[:, h, j0:j0 + 64], rhs=N16[:, h:h + 1], start=True, stop=False),
                                  reads=[QTO, N16O], writes=[PDEN], last=False)
                            sy.mm(lambda: nc.tensor.matmul(pden[:, h:h + 1], lhsT=sw_[:, h, :], rhs=ones16[0:64, 0:1], start=False, stop=True),
                                  reads=[SWO, ONES16], writes=[PDEN, PBIG], last=(h == 7))
                        sy.op('dve', lambda: nc.vector.tensor_scalar(out=dd_[:, 8:16], in0=pden, scalar1=-1.0, scalar2=None, op0=ALU.mult), reads=[PDEN], writes=[DDO])
                        sy.op('dve', lambda: nc.vector.tensor_tensor(out=dd_[:, 0:8], in0=pden, in1=dd_[:, 8:16], op=ALU.max), reads=[PDEN, DDO], writes=[DDO])
                        sy.op('dve', lambda: nc.vector.tensor_tensor(out=dd_[:, 0:8], in0=dd_[:, 0:8], in1=fl3[:, c, :], op=ALU.max), reads=[DDO, FLO], writes=[DDO])
                        sy.op('dve', lambda: nc.vector.reciprocal(out=dd_[:, 8:16], in_=dd_[:, 0:8]), reads=[DDO], writes=[DDO])
                        hh, HHO = hh_[0]
                        sy.op('dve', lambda: nc.vector.tensor_tensor(out=hh[:], in0=pbig[0:64, :].rearrange("p (h v) -> p h v", h=8),
                                                                     in1=dd_[:, 8:16].unsqueeze(2).to_broadcast([64, 8, 256]), op=ALU.mult),
                              reads=[PBIG, DDO], writes=[HHO])
                        if d == 0:
                            sy.dma('act', HF[tok0:tok0 + 64, :], hh[:].rearrange("p h v -> p (h v)"), HHO, reads=[HHO], writes=[SCR])
                        else:
                            hf, HFO = hf_[0]
                            hn, HNO = hn_[hi_ % 2]
                            sy.dma('sp', hf[:].rearrange("p h v -> p (h v)"), HF[tok0:tok0 + 64, :], HFO, reads=[SCR], writes=[HFO])
                            sy.op('pool', lambda: nc.gpsimd.tensor_tensor(out=hf[:], in0=hf[:], in1=hh[:], op=ALU.add), reads=[HFO, HHO], writes=[HFO])
                            sy.op('pool', lambda: nc.gpsimd.tensor_tensor(out=hq_[:], in0=hf[:], in1=hf[:], op=ALU.mult), reads=[HFO], writes=[HQO])
                            sy.op('dve', lambda: nc.vector.tensor_reduce(out=ssm[:, 0:8], in_=hq_[:], axis=AX.X, op=ALU.add), reads=[HQO], writes=[SSM])
                            sy.op('dve', lambda: nc.vector.tensor_scalar(out=ssm[:, 8:16], in0=ssm[:, 0:8], scalar1=1.0 / 256.0, scalar2=EPS, op0=ALU.mult, op1=ALU.add),
                                  reads=[SSM], writes=[SSM])
                            sy.op('act', lambda: nc.scalar.activation(out=ssm[:, 8:16], in_=ssm[:, 8:16], func=AF.Ln), reads=[SSM], writes=[SSM])
                            sy.op('act', lambda: nc.scalar.activation(out=ssm[:, 0:8], in_=ssm[:, 8:16], func=AF.Exp, scale=-0.5), reads=[SSM], writes=[SSM])
                            sy.op('pool', lambda: nc.gpsimd.tensor_tensor(out=hn[:], in0=hf[:], in1=ssm[:, 0:8].unsqueeze(2).to_broadcast([64, 8, 256]), op=ALU.mult),
                                  reads=[HFO, SSM], writes=[HNO])
                            sy.dma('act', HN[tok0:tok0 + 64, :], hn[:].rearrange("p h v -> p (h v)"), HNO, reads=[HNO], writes=[SCR])
                        hi_ += 1
                    for h in range(8):
                        sy.mm(lambda: nc.tensor.matmul(pbig[:, h * 256:(h + 1) * 256], lhsT=ke_[:, h, :], rhs=v16[:, h, :], start=True, stop=True),
                              reads=[KEO, V16O], writes=[PBIG], last=False)
                    for h in range(8):
                        sy.mm(lambda: nc.tensor.matmul(pdn[:, h:h + 1], lhsT=ke_[:, h, :], rhs=ones16[0:64, 0:1], start=True, stop=True),
                              reads=[KEO, ONES16], writes=[PDN, PBIG], last=(h == 7))
                    sy.op('dve', lambda: nc.vector.tensor_tensor(out=C32[:], in0=C32[:], in1=pbig[:].rearrange("p (h v) -> p h v", h=8), op=ALU.add),
                          reads=[C32O, PBIG], writes=[C32O])
                    sy.op('dve', lambda: nc.vector.tensor_tensor(out=N32[:], in0=N32[:], in1=pdn, op=ALU.add), reads=[N32O, PDN], writes=[N32O])
                sy.barrier([SCR])
            sy.barrier([SCR] + [o for (_, o) in vs + hh_ + hf_ + hn_ + pre])

        with ExitStack() as ph:
            def T(name, shape, dt, space='sb'):
                if space == 'sb':
                    t = ph.enter_context(nc.sbuf_tensor(_uniq(name), list(shape), dt))
                else:
                    t = ph.enter_context(nc.psum_tensor(_uniq(name), list(shape), dt))
                return t, Obj(name)
            NKB = NT // 128
            kts = [T("kts%d" % i, [128, NT], BF16) for i in range(2)]
            vbs = [T("vbs%d" % i, [128, NKB, 128], BF16) for i in range(2)]
            qts = [T("qts%d" % i, [128, 512], BF16) for i in range(2)]
            zbs = [T("zbs%d" % i, [128, 512], BF16) for i in range(2)]
            pts_ = [T("ptt%d" % i, [128, 2, 512], BF16) for i in range(3)]
            p2s = [T("p2s%d" % i, [128, 512], BF16) for i in range(3)]
            rl_, RLO = T("rl", [128, 512], F32)
            on_, ONO = T("on", [128, 512], F32)
            gbs = [T("gbs%d" % i, [128, 512], BF16) for i in range(2)]
            pss = [T("pss%d" % i, [128, 2, 512], F32, 'ps') for i in range(2)]
            pos = [T("pos%d" % i, [128, 512], F32, 'ps') for i in range(2)]
            pls = [T("pls%d" % i, [128, 512], F32, 'ps') for i in range(2)]
            VB_v = VB.rearrange("(n p) c -> p n c", p=128)
            sc_att = float(HD_B ** -0.5)
            tasks = []
            unit = 0
            for g in range(NKV_B):
                for hq in range(4):
                    for qg in range(OWN // 512):
                        kb = 0
                        while kb < NKB:
                            nb = min(2, NKB - kb)
                            tasks.append(dict(g=g, head=g * 4 + hq, qg=qg, kb=kb, nb=nb, first=(kb == 0), last=(kb + nb >= NKB),
                                              unit=unit, newg=(kb == 0 and hq == 0 and qg == 0)))
                            kb += nb
                        unit += 1

            def front(i, t):
                g = t['g']
                kt, KTO = kts[g % 2]
                vb, VBO = vbs[g % 2]
                u = t['unit']
                qt, QTO = qts[u % 2]
                zb, ZBO = zbs[u % 2]
                if t['newg']:
                    sy.dma('sp', kt[:], KBT[g * 128:(g + 1) * 128, :], KTO, reads=[SCR], writes=[KTO])
                    for n0 in range(0, NKB, 8):
                        n1 = min(NKB, n0 + 8)
                        sy.dma('sp', vb[:, n0:n1, :], VB_v[:, n0:n1, g * 128:(g + 1) * 128], VBO, reads=[SCR], writes=[VBO])
                if t['first']:
                    head, qg = t['head'], t['qg']
                    sy.dma('sp', qt[:], QBT[head * 128:(head + 1) * 128, qg * 512:(qg + 1) * 512], QTO, reads=[SCR], writes=[QTO])
                    sy.dma('sp', zb[:], SG[2 * 2048 + head * 128:2 * 2048 + (head + 1) * 128, qg * 512:(qg + 1) * 512], ZBO, reads=[SCR], writes=[ZBO])
                ps_, PSO = pss[i % 2]
                pt, PTO = pts_[i % 3]
                p2, P2O = p2s[i % 3]
                kb, nb = t['kb'], t['nb']
                for j in range(nb):
                    sy.mm(lambda: nc.tensor.matmul(ps_[:, j, :], lhsT=kt[:, (kb + j) * 128:(kb + j + 1) * 128], rhs=qt[:], start=True, stop=True),
                          reads=[KTO, QTO], writes=[PSO], last=(j == nb - 1))
                sy.op('act', lambda: nc.scalar.activation(out=pt[:, 0:nb, :], in_=ps_[:, 0:nb, :], func=AF.Exp, scale=sc_att), reads=[PSO], writes=[PTO])
                if nb == 2:
                    sy.op('dve', lambda: nc.vector.tensor_tensor(out=p2[:], in0=pt[:, 0, :], in1=pt[:, 1, :], op=ALU.add), reads=[PTO], writes=[P2O])

            def back(i, t):
                g = t['g']
                vb, VBO = vbs[g % 2]
                u = t['unit']
                zb, ZBO = zbs[u % 2]
                po, POO = pos[u % 2]
                pl, PLO = pls[u % 2]
                gb, GBO = gbs[u % 2]
                pt, PTO = pts_[i % 3]
                p2, P2O = p2s[i % 3]
                kb, nb = t['kb'], t['nb']
                if nb == 2:
                    srhs, SR = p2[:], P2O
                else:
                    srhs, SR = pt[:, 0, :], PTO
                for j in range(nb):
                    sy.mm(lambda: nc.tensor.matmul(po[:], lhsT=vb[:, kb + j, :], rhs=pt[:, j, :], start=(t['first'] and j == 0), stop=(t['last'] and j == nb - 1)),
                          reads=[VBO, PTO], writes=[POO], last=False)
                sy.mm(lambda: nc.tensor.matmul(pl[:], lhsT=ones16[:], rhs=srhs, start=t['first'], stop=t['last']),
                      reads=[ONES16, SR], writes=[PLO, POO], last=True)
                if t['last']:
                    head, qg = t['head'], t['qg']
                    sy.op('dve', lambda: nc.vector.reciprocal(out=rl_[:], in_=pl[:]), reads=[PLO], writes=[RLO])
                    sy.op('dve', lambda: nc.vector.tensor_tensor(out=on_[:], in0=po[:], in1=rl_[:], op=ALU.mult), reads=[POO, RLO], writes=[ONO])
                    sy.op('pool', lambda: nc.gpsimd.tensor_tensor(out=gb[:], in0=on_[:], in1=zb[:], op=ALU.mult), reads=[ONO, ZBO], writes=[GBO])
                    sy.dma('sp', GB[head * 128:(head + 1) * 128, qg * 512:(qg + 1) * 512], gb[:], GBO, reads=[GBO], writes=[SCR])

            for i in range(len(tasks) + 1):
                if i < len(tasks):
                    front(i, tasks[i])
                if i >= 1:
                    back(i - 1, tasks[i - 1])
            sy.barrier([SCR] + [o for (_, o) in kts + vbs + qts + zbs + gbs])

        with ExitStack() as ph:
            def T(name, shape, dt, space='sb'):
                if space == 'sb':
                    t = ph.enter_context(nc.sbuf_tensor(_uniq(name), list(shape), dt))
                else:
                    t = ph.enter_context(nc.psum_tensor(_uniq(name), list(shape), dt))
                return t, Obj(name)
            mhw, MHW = T("mhw", [128, KC], F32)
            sy.dma('sp', mhw[:], mhw_t, MHW, writes=[MHW])
            hns = [T("hns%d" % i, [128, 4, 2048], BF16) for i in range(2)]
            sos = [T("sos%d" % i, [128, KC, 512], BF16) for i in range(2)]
            szs = [T("szs%d" % i, [128, KC, 512], BF16) for i in range(2)]
            gas = [T("gas%d" % i, [128, KC, 512], BF16) for i in range(2)]
            ptr5 = [T("ptr5_%d" % i, [128, 512], BF16, 'ps') for i in range(3)]
            HN_v = HN.rearrange("(t p) f -> p t f", p=128)
            SG_v = SG.rearrange("(c p) t -> p c t", p=128)
            GA_v = GA.rearrange("(c p) t -> p c t", p=128)
            pi = 0
            for gi, (t0, n) in enumerate(_groups(0, OWN)):
                hn, HNO = hns[gi % 2]
                so, SOO = sos[gi % 2]
                sz, SZO = szs[gi % 2]
                ga, GAO = gas[gi % 2]
                sy.dma('sp', hn[:], HN_v[:, t0 // 128:t0 // 128 + 4, :], HNO, reads=[SCR], writes=[HNO])
                sy.dma('act', so[:], SG_v[:, 0:16, t0:t0 + 512], SOO, reads=[SCR], writes=[SOO])
                sy.dma('act', sz[:], SG_v[:, 16:32, t0:t0 + 512], SZO, reads=[SCR], writes=[SZO])
                for k in range(KC):
                    pt, PTO = ptr5[pi % 3]
                    pi += 1
                    for tt in range(4):
                        sy.mm(lambda: nc.tensor.transpose(out=pt[:, tt * 128:(tt + 1) * 128], in_=hn[:, tt, k * 128:(k + 1) * 128], identity=identb[:]),
                              reads=[HNO, IDENTB], writes=[PTO], last=(tt == 3))
                    sy.op('dve', lambda: nc.vector.scalar_tensor_tensor(out=ga[:, k, :], in0=pt[:], scalar=mhw[:, k:k + 1], in1=so[:, k, :], op0=ALU.mult, op1=ALU.mult),
                          reads=[PTO, MHW, SOO], writes=[GAO])
                sy.op('pool', lambda: nc.gpsimd.tensor_tensor(out=ga[:], in0=ga[:], in1=sz[:], op=ALU.mult), reads=[GAO, SZO], writes=[GAO])
                sy.dma('sp', GA_v[:, :, t0:t0 + 512], ga[:], GAO, reads=[GAO], writes=[SCR])
            sy.barrier([SCR] + [o for (_, o) in hns + sos + szs + gas])

        with ExitStack() as ph:
            def T(name, shape, dt, space='sb'):
                if space == 'sb':
                    t = ph.enter_context(nc.sbuf_tensor(_uniq(name), list(shape), dt))
                else:
                    t = ph.enter_context(nc.psum_tensor(_uniq(name), list(shape), dt))
                return t, Obj(name)
            TG = 256
            wa, WAO = T("wa", [128, KC, 1024], BF16)
            wbb_, WBO = T("wbb", [128, KC, 1024], BF16)
            gat = [T("gat%d" % i, [128, KC, TG], BF16) for i in range(2)]
            gbt = [T("gbt%d" % i, [128, KC, TG], BF16) for i in range(2)]
            sga = [T("sga%d" % i, [128, 8, TG], BF16) for i in range(2)]
            sgb = [T("sgb%d" % i, [128, 8, TG], BF16) for i in range(2)]
            mts = [T("mts%d" % i, [128, 8, TG], BF16) for i in range(2)]
            ta_, TAO = T("ta", [128, TG], F32)
            tb_, TBO = T("tb", [128, TG], F32)
            pya = [T("pya%d" % i, [128, 512], F32, 'ps') for i in range(2)]
            pyb = [T("pyb%d" % i, [128, 512], F32, 'ps') for i in range(2)]
            wa_v = w_ba.rearrange("(k p) c -> p k c", p=128)
            wb_v = w_bb.rearrange("(k p) c -> p k c", p=128)
            GA_v = GA.rearrange("(c p) t -> p c t", p=128)
            GB_v = GB.rearrange("(c p) t -> p c t", p=128)
            SG_v = SG.rearrange("(c p) t -> p c t", p=128)
            MT_v = MT.rearrange("(c p) t -> p c t", p=128)
            gi = 0
            ip = 0
            for cb in range(2):
                sy.dma('pool', wa[:], wa_v[:, :, cb * 1024:(cb + 1) * 1024], WAO, writes=[WAO])
                sy.dma('pool', wbb_[:], wb_v[:, :, cb * 1024:(cb + 1) * 1024], WBO, writes=[WBO])
                for (t0, n) in _groups(0, OWN, TG):
                    gaT, GAT = gat[gi % 2]
                    gbT, GBT = gbt[gi % 2]
                    sa, SAO = sga[gi % 2]
                    sb2, SBO = sgb[gi % 2]
                    mt, MTO = mts[gi % 2]
                    gi += 1
                    sy.dma('sp', gaT[:], GA_v[:, :, t0:t0 + TG], GAT, reads=[SCR], writes=[GAT])
                    sy.dma('sp', gbT[:], GB_v[:, :, t0:t0 + TG], GBT, reads=[SCR], writes=[GBT])
                    sy.dma('act', sa[:], SG_v[:, 48 + cb * 8:48 + cb * 8 + 8, t0:t0 + TG], SAO, reads=[SCR], writes=[SAO])
                    sy.dma('act', sb2[:], SG_v[:, 64 + cb * 8:64 + cb * 8 + 8, t0:t0 + TG], SBO, reads=[SCR], writes=[SBO])
                    for ct in range(8):
                        pa, PAO = pya[ip % 2]
                        pb, PBO = pyb[ip % 2]
                        ip += 1
                        for k in range(KC):
                            sy.mm(lambda: nc.tensor.matmul(pa[:, 0:TG], lhsT=wa[:, k, ct * 128:(ct + 1) * 128], rhs=gaT[:, k, :], start=(k == 0), stop=(k == KC - 1)),
                                  reads=[WAO, GAT], writes=[PAO], last=(k == KC - 1))
                        for k in range(KC):
                            sy.mm(lambda: nc.tensor.matmul(pb[:, 0:TG], lhsT=wbb_[:, k, ct * 128:(ct + 1) * 128], rhs=gbT[:, k, :], start=(k == 0), stop=(k == KC - 1)),
                                  reads=[WBO, GBT], writes=[PBO], last=(k == KC - 1))
                        sy.op('dve', lambda: nc.vector.tensor_tensor(out=ta_[:], in0=pa[:, 0:TG], in1=sa[:, ct, :], op=ALU.mult), reads=[PAO, SAO], writes=[TAO])
                        sy.op('dve', lambda: nc.vector.tensor_tensor(out=tb_[:], in0=pb[:, 0:TG], in1=sb2[:, ct, :], op=ALU.mult), reads=[PBO, SBO], writes=[TBO])
                        sy.op('pool', lambda: nc.gpsimd.tensor_tensor(out=mt[:, ct, :], in0=ta_[:], in1=tb_[:], op=ALU.add), reads=[TAO, TBO], writes=[MTO])
                    sy.dma('act', MT_v[:, cb * 8:cb * 8 + 8, t0:t0 + TG], mt[:], MTO, reads=[MTO], writes=[SCR])
            sy.barrier([SCR] + [o for (_, o) in gat + gbt + sga + sgb + mts] + [WAO, WBO])

        with ExitStack() as ph:
            def T(name, shape, dt, space='sb'):
                if space == 'sb':
                    t = ph.enter_context(nc.sbuf_tensor(_uniq(name), list(shape), dt))
                else:
                    t = ph.enter_context(nc.psum_tensor(_uniq(name), list(shape), dt))
                return t, Obj(name)
            wo, WOO = T("wo", [128, KC, 2048], BF16)
            lnwb, LNWB = T("lnwb", [128, D], F32)
            lnbb, LNBB = T("lnbb", [128, D], F32)
            mtl = [T("mtl%d" % i, [128, KC, 512], BF16) for i in range(2)]
            xts = [T("xo%d" % i, [128, D], F32) for i in range(2)]
            tts = [T("to%d" % i, [128, D], F32) for i in range(2)]
            yts = [T("yo%d" % i, [128, D], F32) for i in range(2)]
            st, STO = T("sto", [128, 4, 6], F32)
            mv, MVO = T("mvo", [128, 4], F32)
            pos_ = [T("po5_%d" % i, [128, 2048], F32, 'ps') for i in range(2)]
            wo_v = w_out.rearrange("(k p) c -> p k c", p=128)
            MT_v = MT.rearrange("(c p) t -> p c t", p=128)
            for cb in range(2):
                sy.dma('pool', wo[:, :, cb * 1024:(cb + 1) * 1024], wo_v[:, :, cb * 1024:(cb + 1) * 1024], WOO, writes=[WOO])
            sy.dma('sp', lnwb[:], ln_w.partition_broadcast(128), LNWB, writes=[LNWB])
            sy.dma('sp', lnbb[:], ln_b.partition_broadcast(128), LNBB, writes=[LNBB])
            ti = 0
            outs = []
            for gi, (t0, n) in enumerate(_groups(0, OWN)):
                ml, MLO = mtl[gi % 2]
                sy.dma('sp', ml[:], MT_v[:, :, t0:t0 + 512], MLO, reads=[SCR], writes=[MLO])
                for tt in range(4):
                    xt, XTO = xts[ti % 2]
                    tt_, TTO = tts[ti % 2]
                    yt, YTO = yts[ti % 2]
                    po, POO = pos_[ti % 2]
                    ti += 1
                    r0 = t0 + tt * 128
                    sy.dma('act', xt[:], xa[r0:r0 + 128, :], XTO, writes=[XTO])
                    for cg in range(4):
                        for k in range(KC):
                            sy.mm(lambda: nc.tensor.matmul(po[:, cg * 512:(cg + 1) * 512], lhsT=ml[:, k, tt * 128:(tt + 1) * 128], rhs=wo[:, k, cg * 512:(cg + 1) * 512],
                                                           start=(k == 0), stop=(k == KC - 1)),
                                  reads=[MLO, WOO], writes=[POO], last=(k == KC - 1 and cg == 3))
                    sy.op('dve', lambda: nc.vector.tensor_tensor(out=tt_[:], in0=po[:], in1=gate_b[:], op=ALU.mult), reads=[POO, GATEB], writes=[TTO])
                    sy.op('dve', lambda: nc.vector.scalar_tensor_tensor(out=tt_[:], in0=xt[:], scalar=float(ALPHA), in1=tt_[:], op0=ALU.mult, op1=ALU.add),
                          reads=[XTO, TTO], writes=[TTO])
                    for j in range(4):
                        sy.op('dve', lambda: nc.vector.bn_stats(out=st[:, j, :], in_=tt_[:, j * 512:(j + 1) * 512]), reads=[TTO], writes=[STO])
                    sy.op('dve', lambda: nc.vector.bn_aggr(out=mv[:, 0:2], in_=st[:].rearrange("p a b -> p (a b)")), reads=[STO], writes=[MVO])
                    sy.op('dve', lambda: nc.vector.tensor_scalar(out=mv[:, 2:3], in0=mv[:, 1:2], scalar1=EPS, scalar2=None, op0=ALU.add), reads=[MVO], writes=[MVO])
                    sy.op('act', lambda: nc.scalar.activation(out=mv[:, 2:3], in_=mv[:, 2:3], func=AF.Ln), reads=[MVO], writes=[MVO])
                    sy.op('act', lambda: nc.scalar.activation(out=mv[:, 2:3], in_=mv[:, 2:3], func=AF.Exp, scale=-0.5), reads=[MVO], writes=[MVO])
                    sy.op('dve', lambda: nc.vector.scalar_tensor_tensor(out=mv[:, 3:4], in0=mv[:, 0:1], scalar=-1.0, in1=mv[:, 2:3], op0=ALU.mult, op1=ALU.mult),
                          reads=[MVO], writes=[MVO])
                    sy.op('act', lambda: nc.scalar.activation(out=yt[:], in_=tt_[:], func=AF.Identity, scale=mv[:, 2:3], bias=mv[:, 3:4]), reads=[TTO, MVO], writes=[YTO])
                    sy.op('pool', lambda: nc.gpsimd.tensor_tensor(out=yt[:], in0=yt[:], in1=lnwb[:], op=ALU.mult), reads=[YTO, LNWB], writes=[YTO])
                    sy.op('pool', lambda: nc.gpsimd.tensor_tensor(out=yt[:], in0=yt[:], in1=lnbb[:], op=ALU.add), reads=[YTO, LNBB], writes=[YTO])
                    sy.dma('sp', yout[r0:r0 + 128, :], yt[:], YTO, reads=[YTO])
            sy.finish([o for (_, o) in yts + xts + mtl])
    print("build done: ninst", sy.ninst, "nsem", sy.nsem)
    return nc


def _rope_tables(seq, orig_pos):
    rot_half = HD_B // 2
    inv = (ROPE_THETA ** (-np.arange(0, rot_half, 2, dtype=np.float32) / rot_half)).astype(np.float32)
    row = (orig_pos // GRID_W).astype(np.float32)
    col = (orig_pos % GRID_W).astype(np.float32)
    ar = row[:, None] * inv[None]
    ac = col[:, None] * inv[None]
    cr, sr, cc, scn = np.cos(ar), np.sin(ar), np.cos(ac), np.sin(ac)
    cos = np.concatenate([cr, cr, cc, cc], axis=1).astype(np.float32)
    sin = np.concatenate([-sr, sr, -scn, scn], axis=1).astype(np.float32)
    return cos, sin


_CACHE = {}


def _fm(v, nk):
    return np.ascontiguousarray(np.asarray(v, np.float32).reshape(nk, 128).T)


def kernel(x, c, ctx, c_ctx, w_mod, b_mod, w_in, b_if, conv_w, conv_b, mh_norm_w,
           q_norm_w, k_norm_w, w_branch_a, w_branch_b, w_out, ln_w, ln_b):
    x = np.asarray(x, np.float32)
    ctx = np.asarray(ctx, np.float32)
    B, S, _ = x.shape
    CT = ctx.shape[1]
    OWN = S // 2
    OTH = S - OWN
    key = (OWN, OTH, CT)
    if key not in _CACHE:
        _CACHE[key] = build(OWN, OTH, CT)
    nc = _CACHE[key]
    w_mod0 = np.ascontiguousarray(np.asarray(w_mod, np.float32)[0])
    b_mod0 = np.asarray(b_mod, np.float32)[0]
    w_in0 = np.ascontiguousarray(np.asarray(w_in, np.float32)[0])
    perm = np.arange(N_IN)
    perm[C_IF:C_IF + 16] = np.arange(C_IF + 16, C_IF + 32)
    perm[C_IF + 16:C_IF + 32] = np.arange(C_IF, C_IF + 16)
    w_in1 = np.ascontiguousarray(w_in0[:, perm])
    b_if0 = np.asarray(b_if, np.float32)[0]
    b_if1 = np.concatenate([b_if0[16:32], b_if0[0:16]])
    conv_w0 = np.asarray(conv_w, np.float32)[0]
    conv_b0 = np.asarray(conv_b, np.float32)[0]
    ident = np.eye(128, dtype=np.float32)
    t = np.arange(64)
    Uf = (t[:, None] <= t[None, :]).astype(np.float32)
    Ub = (t[:, None] >= t[None, :]).astype(np.float32)
    tri = np.stack([Uf, Ub, Uf, Ub]).astype(np.float32)
    pos = np.arange(S)
    common = {
        "w_mod": w_mod0,
        "bmod_t": _fm(b_mod0, 48),
        "convb_t": _fm(conv_b0, KC),
        "mhw_t": _fm(np.asarray(mh_norm_w, np.float32)[0], KC),
        "qnw": np.asarray(q_norm_w, np.float32)[0].reshape(1, 128),
        "knw": np.asarray(k_norm_w, np.float32)[0].reshape(1, 128),
        "w_ba": np.ascontiguousarray(np.asarray(w_branch_a, np.float32)[0]),
        "w_bb": np.ascontiguousarray(np.asarray(w_branch_b, np.float32)[0]),
        "w_out": np.ascontiguousarray(np.asarray(w_out, np.float32)[0]),
        "ln_w": np.asarray(ln_w, np.float32)[0].reshape(1, D),
        "ln_b": np.asarray(ln_b, np.float32)[0].reshape(1, D),
        "ident": ident,
        "tri": tri,
    }
    bmod_g = np.ascontiguousarray(np.stack([b_mod0[2 * D:3 * D], b_mod0[2 * D:3 * D]]))
    cos_tabs = {}
    for half in range(2):
        orig = pos if half == 0 else pos[::-1]
        cs, sn = _rope_tables(S, orig)
        cs = np.concatenate([cs, np.ones((CT, 128), np.float32)], axis=0)
        sn = np.concatenate([sn, np.zeros((CT, 128), np.float32)], axis=0)
        cos_tabs[half] = (np.ascontiguousarray(cs), np.ascontiguousarray(sn))
    in_maps = []
    for b in range(B):
        for half in range(2):
            if half == 0:
                xl = x[b]
                cl = ctx[b]
                cw = conv_w0
            else:
                xl = x[b][::-1]
                cl = ctx[b][::-1]
                cw = conv_w0[::-1]
            xa = np.ascontiguousarray(np.concatenate([xl, cl], axis=0))
            cvec = np.ascontiguousarray(np.stack([_fm(np.asarray(c, np.float32)[b], KC), _fm(np.asarray(c_ctx, np.float32), KC)], axis=2))
            convw_t = np.ascontiguousarray(np.stack([_fm(cw[j], KC) for j in range(3)], axis=2))
            m = dict(common)
            m.update({
                "xa": xa, "cvec": cvec, "bmod_g": bmod_g,
                "w_in": w_in0 if half == 0 else w_in1,
                "b_if": (b_if0 if half == 0 else b_if1).reshape(1, 32).astype(np.float32),
                "convw_t": convw_t,
                "rope_c": cos_tabs[half][0], "rope_s": cos_tabs[half][1],
            })
            in_maps.append(m)
    ncores = len(in_maps)
    res = run_bass_kernel_spmd(nc, in_maps, core_ids=list(range(ncores)))
    out = np.empty((B, S, D), np.float32)
    for b in range(B):
        y0 = res.results[2 * b]["yout"]
        y1 = res.results[2 * b + 1]["yout"]
        out[b, :OWN] = y0
        out[b, OWN:] = y1[::-1]
    return out
```
